# Optimizing a Trainium2 kernel written in Bass

```python
import math
import jax, jax.numpy as jnp
from jax import lax
import numpy as np

D_MODEL = 4096
BATCH = 4
SEQ = 2048
DEPTH = 2

MIX_WIDTH = D_MODEL
ATTN_WIDTH = MIX_WIDTH // 2
HYENA_WIDTH = MIX_WIDTH - ATTN_WIDTH
HEAD_DIM = 128
N_HEADS = ATTN_WIDTH // HEAD_DIM
DILATED_CONFIGS = ((128, 1), (512, 4), (2048, 16))
ROPE_THETA = 10000.0
HYENA_ORDER = 2
HYENA_GROUP = 128
N_HYENA_GROUPS = HYENA_WIDTH // HYENA_GROUP
SHORT_CONV = 3
FILTER_EMB_DIM = 33
FILTER_HIDDEN = 64
DECAY_FAST = 0.3
DECAY_SLOW = 1.5
DECAY_TARGET = 1e-2
FFN_HIDDEN = 256 * (-(-(8 * D_MODEL) // (3 * 256)))
IN_COLS = 3 * ATTN_WIDTH + (HYENA_ORDER + 1) * HYENA_WIDTH
RMS_EPS = 1e-6
NEG_INF = -1e30

kernel_name = "hymba_dilated_attn_hyena_macaron"


def _rms(x, g, eps=RMS_EPS):
    xf = x.astype(jnp.float32)
    y = xf * lax.rsqrt(jnp.mean(xf * xf, axis=-1, keepdims=True) + eps)
    return y * g.astype(jnp.float32)


def _group_rms(x, g, group):
    shp = x.shape
    xg = x.reshape(shp[:-1] + (shp[-1] // group, group))
    return _rms(xg, g.reshape(shp[-1] // group, group)).reshape(shp)


def _swiglu(h, wg, wu, wd):
    return (jax.nn.silu(h @ wg) * (h @ wu)) @ wd


def _rotary(x, pos):
    half = x.shape[-1] // 2
    inv = ROPE_THETA ** (-jnp.arange(half, dtype=jnp.float32) / half)
    ang = pos[:, None] * inv[None, :]
    cos = jnp.cos(ang)[None, :, None, :]
    sin = jnp.sin(ang)[None, :, None, :]
    x1, x2 = x[..., :half], x[..., half:]
    return jnp.concatenate([x1 * cos - x2 * sin, x2 * cos + x1 * sin], axis=-1)


def _dilated_window_attention(q, k, v, window, dil):
    B, S, H, Dh = q.shape
    half = window // (2 * dil)
    blk = half
    unit = dil * blk
    Sp = -(-S // unit) * unit
    L = Sp // dil
    nb = L // blk
    pad = ((0, 0), (0, Sp - S), (0, 0), (0, 0))

    def to_blocks(a):
        a = jnp.pad(a, pad)
        return a.reshape(B, L, dil, H, Dh).transpose(0, 2, 1, 3, 4).reshape(B, dil, nb, blk, H, Dh)

    def with_neighbours(a):
        ap = jnp.pad(a, [(0, 0), (0, 0), (1, 1)] + [(0, 0)] * (a.ndim - 3))
        return jnp.concatenate([ap[:, :, :-2], ap[:, :, 1:-1], ap[:, :, 2:]], axis=3)

    qb = to_blocks(q)
    kb = with_neighbours(to_blocks(k))
    vb = with_neighbours(to_blocks(v))
    valid = (jnp.arange(Sp) < S).reshape(L, dil).T.reshape(1, dil, nb, blk)
    kvalid = with_neighbours(valid)
    rel = jnp.arange(3 * blk)[None, :] - blk - jnp.arange(blk)[:, None]
    band = jnp.abs(rel) <= half
    mask = band[None, None, None, None] & kvalid[:, :, :, None, None, :]

    s = jnp.einsum('brnqhd,brnkhd->brnhqk', qb, kb) / math.sqrt(Dh)
    s = jnp.where(mask, s, NEG_INF)
    m = jnp.max(s, axis=-1, keepdims=True)
    p = jnp.exp(s - m)
    den = jnp.sum(p, axis=-1, keepdims=True)
    o = jnp.einsum('brnhqk,brnkhd->brnqhd', p / den, vb)
    lse = (m + jnp.log(den))[..., 0].transpose(0, 1, 2, 4, 3)
    o = o.reshape(B, dil, L, H, Dh).transpose(0, 2, 1, 3, 4).reshape(B, Sp, H, Dh)[:, :S]
    lse = lse.reshape(B, dil, L, H).transpose(0, 2, 1, 3).reshape(B, Sp, H)[:, :S]
    return o, lse


def _dilated_attention_mixer(q, k, v):
    outs, lses = [], []
    for window, dil in DILATED_CONFIGS:
        o, l = _dilated_window_attention(q, k, v, window, dil)
        outs.append(o)
        lses.append(l)
    w = jax.nn.softmax(jnp.stack(lses, axis=0), axis=0)
    return jnp.sum(w[..., None] * jnp.stack(outs, axis=0), axis=0)


def _hyena_filters(L, w1, b1, w2, b2, freq, w3):
    f32 = jnp.float32
    t = jnp.linspace(0.0, 1.0, L, dtype=f32)[:, None]
    bands = (FILTER_EMB_DIM - 1) // 2
    f = jnp.linspace(1e-4, bands - 1, bands, dtype=f32)[None, :]
    wpos = (2.0 * math.pi) * jnp.arange(L, dtype=f32)[:, None] / L
    emb = jnp.concatenate([t, jnp.cos(f * wpos), -jnp.sin(f * wpos)], axis=-1)
    freq = freq.astype(f32)
    h = jnp.sin(freq[0] * (emb @ w1.astype(f32) + b1.astype(f32)))
    h = jnp.sin(freq[1] * (h @ w2.astype(f32) + b2.astype(f32)))
    h = h @ w3.astype(f32)
    deltas = jnp.abs(jnp.linspace(math.log(DECAY_FAST) / DECAY_TARGET,
                                  math.log(DECAY_SLOW) / DECAY_TARGET, HYENA_WIDTH, dtype=f32))
    decay = jnp.exp(-t * deltas[None, :])
    return h.reshape(L, HYENA_ORDER, 2, HYENA_WIDTH) * decay[:, None, None, :]


def _bidirectional_long_conv(z, h_fwd, h_bwd, bias):
    L, C = h_fwd.shape
    k2 = jnp.concatenate([h_fwd, jnp.zeros((1, C), h_fwd.dtype), h_bwd[1:][::-1]], axis=0)
    Z = jnp.fft.rfft(z, n=2 * L, axis=1)
    K = jnp.fft.rfft(k2, axis=0)
    y = jnp.fft.irfft(Z * K[None], n=2 * L, axis=1)[:, :L]
    return y + bias.astype(jnp.float32) * z


def _hyena_mixer(u, conv_w, conv_b, w1, b1, w2, b2, freq, w3, filt_bias):
    S = u.shape[1]
    up = jnp.pad(u, ((0, 0), (1, 1), (0, 0)))
    u = (conv_w[0] * up[:, :-2] + conv_w[1] * up[:, 1:-1] + conv_w[2] * up[:, 2:] + conv_b).astype(jnp.float32)
    z = u[..., :HYENA_WIDTH]
    gates = [u[..., (n + 1) * HYENA_WIDTH:(n + 2) * HYENA_WIDTH] for n in range(HYENA_ORDER)]
    filt = _hyena_filters(S, w1, b1, w2, b2, freq, w3)
    for n in range(HYENA_ORDER):
        z = gates[n] * _bidirectional_long_conv(z, filt[:, n, 0], filt[:, n, 1], filt_bias[n])
    return z


def setup_inputs(seed: int = 0) -> dict:
    key = jax.random.key(seed)
    ks = jax.random.split(key, 24)
    f32 = jnp.float32
    D, F, A, C = D_MODEL, FFN_HIDDEN, ATTN_WIDTH, HYENA_WIDTH
    nrm = lambda k, shape, scale: jax.random.normal(k, shape, f32) * scale
    return {
        "x": nrm(ks[0], (BATCH, SEQ, D), 1.0),
        "ffn_norm": 1.0 + nrm(ks[1], (DEPTH, 2, D), 0.02),
        "ffn_w_gate": nrm(ks[2], (DEPTH, 2, D, F), D ** -0.5),
        "ffn_w_up": nrm(ks[3], (DEPTH, 2, D, F), D ** -0.5),
        "ffn_w_down": nrm(ks[4], (DEPTH, 2, F, D), F ** -0.5),
        "mix_norm": 1.0 + nrm(ks[5], (DEPTH, D), 0.02),
        "w_in": nrm(ks[6], (DEPTH, D, IN_COLS), D ** -0.5),
        "q_norm": 1.0 + nrm(ks[7], (DEPTH, HEAD_DIM), 0.02),
        "k_norm": 1.0 + nrm(ks[8], (DEPTH, HEAD_DIM), 0.02),
        "conv_w": nrm(ks[9], (DEPTH, SHORT_CONV, (HYENA_ORDER + 1) * C), SHORT_CONV ** -0.5),
        "conv_b": nrm(ks[10], (DEPTH, (HYENA_ORDER + 1) * C), 0.02),
        "filt_w1": nrm(ks[11], (DEPTH, FILTER_EMB_DIM, FILTER_HIDDEN), FILTER_EMB_DIM ** -0.5),
        "filt_b1": nrm(ks[12], (DEPTH, FILTER_HIDDEN), 0.02),
        "filt_w2": nrm(ks[13], (DEPTH, FILTER_HIDDEN, FILTER_HIDDEN), FILTER_HIDDEN ** -0.5),
        "filt_b2": nrm(ks[14], (DEPTH, FILTER_HIDDEN), 0.02),
        "filt_freq": 1.0 + nrm(ks[15], (DEPTH, 2, FILTER_HIDDEN), 0.02),
        "filt_w3": nrm(ks[16], (DEPTH, FILTER_HIDDEN, HYENA_ORDER * 2 * C), FILTER_HIDDEN ** -0.5),
        "filt_bias": nrm(ks[17], (DEPTH, HYENA_ORDER, C), 1.0),
        "attn_out_norm": 1.0 + nrm(ks[18], (DEPTH, A), 0.02),
        "hyena_out_norm": 1.0 + nrm(ks[19], (DEPTH, C), 0.02),
        "w_out": nrm(ks[20], (DEPTH, MIX_WIDTH, D), MIX_WIDTH ** -0.5),
    }


def reference(x, ffn_norm, ffn_w_gate, ffn_w_up, ffn_w_down, mix_norm, w_in, q_norm, k_norm,
              conv_w, conv_b, filt_w1, filt_b1, filt_w2, filt_b2, filt_freq, filt_w3, filt_bias,
              attn_out_norm, hyena_out_norm, w_out):
    B, S, _ = x.shape
    pos = jnp.arange(S, dtype=jnp.float32)
    A = ATTN_WIDTH
    for l in range(DEPTH):
        h = _rms(x, ffn_norm[l, 0]).astype(x.dtype)
        x = x + 0.5 * _swiglu(h, ffn_w_gate[l, 0], ffn_w_up[l, 0], ffn_w_down[l, 0])

        h = _rms(x, mix_norm[l]).astype(x.dtype)
        proj = h @ w_in[l]
        q = proj[..., :A].reshape(B, S, N_HEADS, HEAD_DIM)
        k = proj[..., A:2 * A].reshape(B, S, N_HEADS, HEAD_DIM)
        v = proj[..., 2 * A:3 * A].reshape(B, S, N_HEADS, HEAD_DIM).astype(jnp.float32)
        q = _rotary(_rms(q, q_norm[l]), pos)
        k = _rotary(_rms(k, k_norm[l]), pos)
        attn = _dilated_attention_mixer(q, k, v).reshape(B, S, A)

        hy = _hyena_mixer(proj[..., 3 * A:], conv_w[l], conv_b[l], filt_w1[l], filt_b1[l],
                          filt_w2[l], filt_b2[l], filt_freq[l], filt_w3[l], filt_bias[l])

        merged = jnp.concatenate([_group_rms(attn, attn_out_norm[l], HEAD_DIM),
                                  _group_rms(hy, hyena_out_norm[l], HYENA_GROUP)], axis=-1)
        x = x + merged.astype(x.dtype) @ w_out[l]

        h = _rms(x, ffn_norm[l, 1]).astype(x.dtype)
        x = x + 0.5 * _swiglu(h, ffn_w_gate[l, 1], ffn_w_up[l, 1], ffn_w_down[l, 1])
    return x
```

```python
import contextlib
import numpy as np
import concourse.bass as bass
import concourse.mybir as mybir
from concourse.bass_utils import run_bass_kernel_spmd

F32 = mybir.dt.float32
BF16 = mybir.dt.bfloat16
AF = mybir.ActivationFunctionType
ALU = mybir.AluOpType
AX = mybir.AxisListType

RMS_EPS = 1e-6


class Buf:
    __slots__ = ("name", "w", "r")

    def __init__(self, name=""):
        self.name = name
        self.w = None
        self.r = []


class Op:
    __slots__ = ("eng", "fn", "comp", "deps", "value", "needs_inc", "idx", "is_dma")

    def __init__(self, eng, fn, comp, is_dma):
        self.eng = eng
        self.fn = fn
        self.comp = comp
        self.deps = []
        self.value = None
        self.needs_inc = is_dma
        self.is_dma = is_dma


ENGS = ("pe", "act", "dve", "pool", "sp")


class Prog:
    def __init__(self, nc):
        self.nc = nc
        self.streams = {e: [] for e in ENGS}
        self.last_on_comp = {}
        self.n_ops = 0

    def add(self, eng, fn, reads=(), writes=(), dma=None):
        comp = ("dma:" + dma) if dma is not None else eng
        op = Op(eng, fn, comp, dma is not None)
        deps = {}

        def dep(o):
            if o is None or o is op:
                return
            if o.comp == "pe" and eng == "pe" and not op.is_dma:
                return
            cur = deps.get(o.comp)
            if cur is None or cur.idx < o.idx:
                deps[o.comp] = o

        for b in reads:
            dep(b.w)
        for b in writes:
            dep(b.w)
            for r in b.r:
                dep(r)
        if op.is_dma:
            dep(self.last_on_comp.get(comp))
        op.idx = self.n_ops
        self.n_ops += 1
        op.deps = list(deps.values())
        for b in reads:
            b.r.append(op)
        for b in writes:
            b.w = op
            b.r = []
        self.last_on_comp[comp] = op
        self.streams[eng].append(op)
        return op

    def barrier(self):
        lasts = [o for o in self.last_on_comp.values() if o.fn is not None]
        for e in ENGS:
            op = Op(e, None, e, False)
            op.idx = self.n_ops
            self.n_ops += 1
            op.deps = [o for o in lasts if not (o.comp == "pe" and e == "pe")]
            self.streams[e].append(op)

    def emit(self):
        nc = self.nc
        for e in ENGS:
            for op in self.streams[e]:
                for d in op.deps:
                    d.needs_inc = True
        counters = {}
        allops = sorted((op for e in ENGS for op in self.streams[e]), key=lambda o: o.idx)
        for op in allops:
            if op.needs_inc and op.fn is not None:
                step = 16 if op.is_dma else 1
                counters[op.comp] = counters.get(op.comp, 0) + step
                op.value = counters[op.comp]
        self.maxvals = dict(counters)
        with contextlib.ExitStack() as st:
            sems = {c: st.enter_context(nc.semaphore("s_" + c.replace(":", "_")))
                    for c in sorted(counters.keys())}
            block = st.enter_context(nc.Block())

            def run(engname):
                def body(e):
                    seen = {}
                    for op in self.streams[engname]:
                        for d in op.deps:
                            if d.fn is None or seen.get(d.comp, 0) >= d.value:
                                continue
                            e.wait_ge(sems[d.comp], d.value)
                            seen[d.comp] = d.value
                        if op.fn is None:
                            continue
                        ins = op.fn(e)
                        if op.needs_inc:
                            ins.then_inc(sems[op.comp], 16 if op.is_dma else 1)
                return body

            block.tensor(run("pe"))
            block.scalar(run("act"))
            block.vector(run("dve"))
            block.gpsimd(run("pool"))
            block.sync(run("sp"))


class Arena:
    def __init__(self, base_ap, nbytes):
        self.base = base_ap
        self.nbytes = nbytes
        self.off = 0

    def alloc(self, shape_free, dtype):
        esz = 2 if dtype == BF16 else 4
        n = int(np.prod(shape_free))
        nb = (n * esz + 63) // 64 * 64
        assert self.off + nb <= self.nbytes, f"SBUF arena overflow {self.off + nb} > {self.nbytes}"
        v = self.base[:, self.off // 4:(self.off + nb) // 4]
        if dtype == BF16:
            v = v.bitcast(BF16)
        v = v[:, 0:n]
        self.off += nb
        if len(shape_free) == 2:
            v = v.rearrange("p (a b) -> p a b", a=shape_free[0])
        elif len(shape_free) == 3:
            v = v.rearrange("p (a b c) -> p a b c", a=shape_free[0], b=shape_free[1])
        return v

    def mark(self):
        return self.off

    def reset(self, m=0):
        self.off = m


class Ctx:
    pass


def emit_rmsnorm_hT(P, C, x_src, xbufs, g_sb, hT, hT_buf, D, NT):
    nc = P.nc
    KC = D // 128
    NH = NT // 512
    A = C.arena
    m = A.mark()
    xs = [A.alloc([NT], F32) for _ in range(2)]
    xs_b = [Buf() for _ in range(2)]
    sq = [A.alloc([NT], BF16) for _ in range(2)]
    sq_b = [Buf() for _ in range(2)]
    rstd = A.alloc([NT], F32)
    rstd_b = Buf()
    ps = C.psum
    for kc in range(KC):
        s = kc % 2
        P.add("sp", lambda e, s=s, kc=kc: e.dma_start(out=xs[s], in_=x_src[kc * 128:(kc + 1) * 128, :]),
              reads=[xbufs[kc]], writes=[xs_b[s]], dma=f"xs{s}")
        P.add("act", lambda e, s=s: e.activation(out=sq[s], in_=xs[s], func=AF.Square),
              reads=[xs_b[s]], writes=[sq_b[s]])
        for h in range(NH):
            P.add("pe", lambda e, s=s, h=h, kc=kc: e.matmul(ps[h], lhsT=C.ones_bf, rhs=sq[s][:, h * 512:(h + 1) * 512],
                                                         start=(kc == 0), stop=(kc == KC - 1)),
                  reads=[sq_b[s], C.const_b], writes=[C.psum_b[h]])
    for h in range(NH):
        P.add("dve", lambda e, h=h: e.tensor_scalar(out=rstd[:, h * 512:(h + 1) * 512], in0=ps[h],
                                                    scalar1=1.0 / D, scalar2=RMS_EPS, op0=ALU.mult, op1=ALU.add),
              reads=[C.psum_b[h]], writes=[rstd_b])
    P.add("act", lambda e: e.activation(out=rstd, in_=rstd, func=AF.Sqrt),
          reads=[rstd_b], writes=[rstd_b])
    P.add("dve", lambda e: e.reciprocal(out=rstd, in_=rstd),
          reads=[rstd_b], writes=[rstd_b])
    for kc in range(KC):
        s = kc % 2
        P.add("sp", lambda e, s=s, kc=kc: e.dma_start(out=xs[s], in_=x_src[kc * 128:(kc + 1) * 128, :]),
              reads=[xbufs[kc]], writes=[xs_b[s]], dma=f"xs{s}")
        P.add("dve", lambda e, s=s, kc=kc: e.scalar_tensor_tensor(out=hT[:, kc, :], in0=xs[s], scalar=g_sb[:, kc:kc + 1],
                                                                  in1=rstd, op0=ALU.mult, op1=ALU.mult),
              reads=[xs_b[s], rstd_b, C.const_b], writes=[hT_buf])
    P.barrier()
    A.reset(m)


def emit_ffn(P, C, x_src, x_dst, xbufs, g_sb, wg_t, wu_t, wd_t, D, F, NT, NP):
    nc = P.nc
    KC = D // 128
    FC = F // 128
    NH = NT // 512
    A = C.arena
    m0 = A.mark()
    hT = A.alloc([KC, NT], BF16)
    hT_b = Buf()
    emit_rmsnorm_hT(P, C, x_src, xbufs, g_sb, hT, hT_b, D, NT)
    base = FC // NP
    rem = FC % NP
    parts = []
    f0 = 0
    for p in range(NP):
        n = base + (1 if p < rem else 0)
        parts.append((f0, n))
        f0 += n
    nmax = parts[0][1]
    G = A.alloc([nmax, NT], BF16)
    G_b = [Buf() for _ in range(nmax)]
    WSL = 4
    wslot_elems = max(KC * 128, nmax * 128)
    wsl = [A.alloc([wslot_elems], BF16) for _ in range(WSL)]
    wsl_b = [Buf() for _ in range(WSL)]
    sg = [A.alloc([NT], F32) for _ in range(2)]
    sg_b = [Buf() for _ in range(2)]
    xr = [A.alloc([NT], F32) for _ in range(2)]
    xr_b = [Buf() for _ in range(2)]
    xo = [A.alloc([NT], F32) for _ in range(2)]
    xo_b = [Buf() for _ in range(2)]
    ps = C.psum
    psb = C.psum_b
    wctr = [0]

    def wload(src_ap, nel):
        s = wctr[0] % WSL
        wctr[0] += 1
        view = wsl[s][:, 0:nel]
        P.add("pool", lambda e, view=view, src_ap=src_ap: e.dma_start(out=view, in_=src_ap),
              writes=[wsl_b[s]], dma=f"w{s}")
        return s

    cur_src = x_src
    for (pf0, pn) in parts:
        for fl in range(pn):
            fc = pf0 + fl
            sgi = wload(wg_t[fc].rearrange("p k f -> p (k f)"), KC * 128)
            sui = wload(wu_t[fc].rearrange("p k f -> p (k f)"), KC * 128)
            wgv = wsl[sgi][:, 0:KC * 128].rearrange("p (k f) -> p k f", k=KC)
            wuv = wsl[sui][:, 0:KC * 128].rearrange("p (k f) -> p k f", k=KC)
            pb = (fl % 2) * 4
            for (wv, wi, boff) in ((wgv, sgi, 0), (wuv, sui, NH)):
                for kc in range(KC):
                    for h in range(NH):
                        P.add("pe", lambda e, wv=wv, kc=kc, h=h, bank=pb + boff + h: e.matmul(
                            ps[bank], lhsT=wv[:, kc, :], rhs=hT[:, kc, h * 512:(h + 1) * 512],
                            start=(kc == 0), stop=(kc == KC - 1)),
                            reads=[wsl_b[wi], hT_b], writes=[psb[pb + boff + h]])
            s2 = fl % 2
            for h in range(NH):
                P.add("act", lambda e, s2=s2, h=h, bank=pb + h: e.activation(
                    out=sg[s2][:, h * 512:(h + 1) * 512], in_=ps[bank], func=AF.Silu),
                    reads=[psb[pb + h]], writes=[sg_b[s2]])
            for h in range(NH):
                P.add("dve", lambda e, s2=s2, h=h, fl=fl, bank=pb + NH + h: e.tensor_tensor(
                    out=G[:, fl, h * 512:(h + 1) * 512], in0=sg[s2][:, h * 512:(h + 1) * 512], in1=ps[bank], op=ALU.mult),
                    reads=[sg_b[s2], psb[pb + NH + h]], writes=[G_b[fl]])
        for dc in range(KC):
            wi = wload(wd_t[dc, :, pf0:pf0 + pn, :].rearrange("p k f -> p (k f)"), pn * 128)
            wv = wsl[wi][:, 0:pn * 128].rearrange("p (k f) -> p k f", k=pn)
            s2 = dc % 2
            pb = s2 * 4
            P.add("sp", lambda e, s2=s2, dc=dc, cur_src=cur_src: e.dma_start(out=xr[s2], in_=cur_src[dc * 128:(dc + 1) * 128, :]),
                  reads=[xbufs[dc]], writes=[xr_b[s2]], dma=f"xr{s2}")
            for h in range(NH):
                for fl in range(pn):
                    P.add("pe", lambda e, wv=wv, fl=fl, h=h, bank=pb + h: e.matmul(
                        ps[bank], lhsT=wv[:, fl, :], rhs=G[:, fl, h * 512:(h + 1) * 512],
                        start=(fl == 0), stop=(fl == pn - 1)),
                        reads=[wsl_b[wi], G_b[fl]], writes=[psb[pb + h]])
            for h in range(NH):
                P.add("dve", lambda e, s2=s2, h=h, bank=pb + h: e.scalar_tensor_tensor(
                    out=xo[s2][:, h * 512:(h + 1) * 512], in0=ps[bank], scalar=0.5, in1=xr[s2][:, h * 512:(h + 1) * 512],
                    op0=ALU.mult, op1=ALU.add),
                    reads=[psb[pb + h], xr_b[s2]], writes=[xo_b[s2]])
            P.add("sp", lambda e, s2=s2, dc=dc: e.dma_start(out=x_dst[dc * 128:(dc + 1) * 128, :], in_=xo[s2]),
                  reads=[xo_b[s2]], writes=[xbufs[dc]], dma=f"xo{s2}")
        cur_src = x_dst
    P.barrier()
    A.reset(m0)


def make_ctx(nc, st, P):
    C = Ctx()
    nbytes = 200 * 1024
    big = st.enter_context(nc.sbuf_tensor("arena", [128, nbytes // 4], F32))
    C.arena = Arena(big[:], nbytes)
    pst = st.enter_context(nc.psum_tensor("ps", [128, 8 * 512], F32))
    C.psum = [pst[:, b * 512:(b + 1) * 512] for b in range(8)]
    C.psum_b = [Buf(f"ps{b}") for b in range(8)]
    C.ones_bf = C.arena.alloc([128], BF16)
    C.const_b = Buf("const")
    P.add("dve", lambda e: e.memset(C.ones_bf, 1.0), writes=[C.const_b])
    return C


def tile_w_in(w, D):
    KC = D // 128
    N = w.shape[1]
    return np.ascontiguousarray(w.reshape(KC, 128, N // 128, 128).transpose(2, 1, 0, 3))


def build_ffn_prog(D, F, NT, NP):
    nc = bass.Bass("TRN2", target_bir_lowering=False)
    KC, FC = D // 128, F // 128
    x = nc.dram_tensor("x", [D, NT], F32, kind="ExternalInput").ap()
    g = nc.dram_tensor("g", [128, KC], F32, kind="ExternalInput").ap()
    wg = nc.dram_tensor("wg", [FC, 128, KC, 128], F32, kind="ExternalInput").ap()
    wu = nc.dram_tensor("wu", [FC, 128, KC, 128], F32, kind="ExternalInput").ap()
    wd = nc.dram_tensor("wd", [KC, 128, FC, 128], F32, kind="ExternalInput").ap()
    y = nc.dram_tensor("y", [D, NT], F32, kind="ExternalOutput").ap()
    with contextlib.ExitStack() as st:
        st.enter_context(nc.allow_low_precision("bf16 matmul operands, fp32 accumulate (problem spec)"))
        P = Prog(nc)
        C = make_ctx(nc, st, P)
        g_sb = C.arena.alloc([KC], F32)
        P.add("sp", lambda e: e.dma_start(out=g_sb, in_=g), writes=[C.const_b], dma="c")
        xbufs = [Buf() for _ in range(KC)]
        emit_ffn(P, C, x, y, xbufs, g_sb, wg, wu, wd, D, F, NT, NP)
        P.emit()
    return nc


HEAD_DIM = 128
SEQ = 2048
PAD = 1024
DILS = (1, 4, 16)
NEG = -30000.0


def host_consts():
    c = {}
    half = HEAD_DIM // 2
    inv = (10000.0 ** (-np.arange(half, dtype=np.float32) / half)).astype(np.float32)
    pos = np.arange(SEQ, dtype=np.float32)
    ang = (pos[None, :] * np.concatenate([inv, inv])[:, None]).astype(np.float32)
    c["cosT"] = np.cos(ang).astype(np.float32)
    c["sinT"] = np.sin(ang).astype(np.float32)
    rm = np.zeros((128, 128), np.float32)
    for m in range(64):
        rm[m + 64, m] = -1.0
    for m in range(64, 128):
        rm[m - 64, m] = 1.0
    c["rotm"] = rm
    c["ident"] = np.eye(128, dtype=np.float32)
    p = np.arange(128)[:, None]
    n = np.arange(128)[None, :]
    A = (p >= n)
    B = (p <= n)
    A1 = A & (p >= 64)
    B1 = B & (p < 64)
    def bias(m):
        return np.where(m, 0.0, NEG).astype(np.float32)
    c["maskb"] = np.ascontiguousarray(np.stack([np.concatenate([bias(a), bias(b)], 1)
                                                for a, b in ((A, B), (A1, B), (A, B1), (A1, B1))], 1))
    return c


def emit_colnorm(P, C, x, x_b, N, rstd, rstd_b, sq, sq_b, pbanks):
    ps, psb = C.psum, C.psum_b
    P.add("act", lambda e: e.activation(out=sq, in_=x, func=AF.Square), reads=[x_b], writes=[sq_b])
    for h in range(N // 512):
        b = pbanks[h]
        P.add("pe", lambda e, h=h, b=b: e.matmul(ps[b], lhsT=C.ones_bf, rhs=sq[:, h * 512:(h + 1) * 512], start=True, stop=True),
              reads=[sq_b, C.const_b], writes=[psb[b]])
        P.add("dve", lambda e, h=h, b=b: e.tensor_scalar(out=rstd[:, h * 512:(h + 1) * 512], in0=ps[b], scalar1=1.0 / 128,
                                                         scalar2=RMS_EPS, op0=ALU.mult, op1=ALU.add),
              reads=[psb[b]], writes=[rstd_b])
    P.add("act", lambda e: e.activation(out=rstd, in_=rstd, func=AF.Sqrt), reads=[rstd_b], writes=[rstd_b])
    P.add("dve", lambda e: e.reciprocal(out=rstd, in_=rstd), reads=[rstd_b], writes=[rstd_b])


def emit_attention(P, C, pj, pj_b, nheads, K_, out_dram, out_b):
    A = C.arena
    m0 = A.mark()
    ps, psb = C.psum, C.psum_b
    S = SEQ
    xin = [A.alloc([S], F32) for _ in range(3)]
    xin_b = [Buf() for _ in range(3)]
    sq = A.alloc([S], BF16); sq_b = Buf()
    rstd = A.alloc([S], F32); rstd_b = Buf()
    xn = A.alloc([S], F32); xn_b = Buf()
    t1 = A.alloc([S], F32); t1_b = Buf()
    t2 = A.alloc([S], F32); t2_b = Buf()
    qTb = A.alloc([S], BF16); qTb_b = Buf()
    kTb = A.alloc([S + 2 * PAD], BF16); kTb_b = Buf()
    vTb = A.alloc([S + 2 * PAD], BF16); vTb_b = Buf()
    nblk = [d * (S // d // 128 + 1) for d in DILS]
    Vb = [A.alloc([nb, 128], BF16) for nb in nblk]
    Vb_b = [Buf() for _ in DILS]
    PT = [A.alloc([512], BF16) for _ in range(2)]
    PT_b = [Buf() for _ in range(2)]
    accn = A.alloc([S], F32); accn_b = Buf()
    accd = A.alloc([S], F32); accd_b = Buf()
    ob = A.alloc([S], BF16); ob_b = Buf()
    P.add("dve", lambda e: e.memset(kTb, 0.0), writes=[kTb_b])
    P.add("dve", lambda e: e.memset(vTb, 0.0), writes=[vTb_b])
    scale = 1.0 / float(np.sqrt(HEAD_DIM))
    cnt = [0]
    for hd in range(nheads):
        for i3 in range(3):
            P.add("sp", lambda e, i3=i3, hd=hd: e.dma_start(out=xin[i3], in_=pj[i3 * nheads + hd]),
                  reads=[pj_b[i3 * nheads + hd]], writes=[xin_b[i3]], dma=f"ain{i3}")
        for i3, (gname, dst, dst_b, off) in enumerate((("gq", qTb, qTb_b, 0), ("gk", kTb, kTb_b, PAD))):
            emit_colnorm(P, C, xin[i3], xin_b[i3], S, rstd, rstd_b, sq, sq_b, [0, 1, 2, 3])
            P.add("dve", lambda e, i3=i3, gname=gname: e.scalar_tensor_tensor(out=xn, in0=xin[i3], scalar=K_[gname][:, 0:1], in1=rstd,
                                                                              op0=ALU.mult, op1=ALU.mult),
                  reads=[xin_b[i3], rstd_b, C.const_b], writes=[xn_b])
            for h in range(4):
                P.add("pe", lambda e, h=h: e.matmul(ps[4 + h], lhsT=K_["rotm"], rhs=xn[:, h * 512:(h + 1) * 512], start=True, stop=True),
                      reads=[xn_b, C.const_b], writes=[psb[4 + h]])
            P.add("dve", lambda e: e.tensor_tensor(out=t1, in0=xn, in1=K_["cosT"], op=ALU.mult),
                  reads=[xn_b, C.const_b], writes=[t1_b])
            for h in range(4):
                P.add("dve", lambda e, h=h: e.tensor_tensor(out=t2[:, h * 512:(h + 1) * 512], in0=ps[4 + h],
                                                            in1=K_["sinT"][:, h * 512:(h + 1) * 512], op=ALU.mult),
                      reads=[psb[4 + h], C.const_b], writes=[t2_b])
            P.add("dve", lambda e, dst=dst, off=off: e.tensor_tensor(out=dst[:, off:off + S], in0=t1, in1=t2, op=ALU.add),
                  reads=[t1_b, t2_b], writes=[dst_b])
        P.add("act", lambda e: e.activation(out=vTb[:, PAD:PAD + S], in_=xin[2], func=AF.Copy),
              reads=[xin_b[2]], writes=[vTb_b])
        for ci, d in enumerate(DILS):
            L = S // d
            nj = L // 128 + 1
            blocks = [(r, j) for r in range(d) for j in range(nj)]
            for g0 in range(0, len(blocks), 4):
                grp = blocks[g0:g0 + 4]
                bank = cnt[0] % 4
                cnt[0] += 1
                for bi, (r, j) in enumerate(grp):
                    st0 = PAD + (128 * j - 64) * d + r
                    P.add("pe", lambda e, bank=bank, bi=bi, st0=st0, d=d: e.matmul(
                        ps[bank][:, bi * 128:(bi + 1) * 128], lhsT=vTb[:, st0:st0 + 127 * d + 1:d], rhs=K_["ident_b"], start=True, stop=True),
                        reads=[vTb_b, C.const_b], writes=[psb[bank]])
                n = len(grp)
                P.add("act", lambda e, ci=ci, g0=g0, n=n, bank=bank: e.activation(
                    out=Vb[ci][:, g0:g0 + n, :], in_=ps[bank][:, 0:n * 128].rearrange("p (a b) -> p a b", a=n), func=AF.Copy),
                    reads=[psb[bank]], writes=[Vb_b[ci]])
        for ci, d in enumerate(DILS):
            L = S // d
            nqb = L // 128
            nj = nqb + 1
            qblocks = [(r, i) for r in range(d) for i in range(nqb)]
            for g0 in range(0, len(qblocks), 4):
                grp = qblocks[g0:g0 + 4]
                nb_, db_ = 4 + (cnt[0] % 2), 6 + (cnt[0] % 2)
                cnt[0] += 1
                for q0 in range(0, 4, 2):
                    sb_ = cnt[0] % 2
                    pt = cnt[0] % 2
                    cnt[0] += 1
                    for qq in range(2):
                        r, i = grp[q0 + qq]
                        qst = (128 * i) * d + r
                        for kk in range(2):
                            j = i + kk
                            kst = PAD + (128 * j - 64) * d + r
                            col = (qq * 2 + kk) * 128
                            var = (1 if (kk == 0 and j == 0) else 0)
                            if kk == 1 and j == nj - 1:
                                var = 2
                            mslice = K_["maskb"][:, var, kk * 128:(kk + 1) * 128]
                            P.add("pe", lambda e, sb_=sb_, col=col, kst=kst, qst=qst, d=d: e.matmul(
                                ps[sb_][:, col:col + 128], lhsT=kTb[:, kst:kst + 127 * d + 1:d], rhs=qTb[:, qst:qst + 127 * d + 1:d],
                                start=True, stop=False),
                                reads=[kTb_b, qTb_b], writes=[psb[sb_]])
                            P.add("pe", lambda e, sb_=sb_, col=col, mslice=mslice: e.matmul(
                                ps[sb_][:, col:col + 128], lhsT=K_["ident_b"], rhs=mslice, start=False, stop=True),
                                reads=[C.const_b], writes=[psb[sb_]])
                    P.add("act", lambda e, sb_=sb_, pt=pt: e.activation(out=PT[pt], in_=ps[sb_], func=AF.Exp, scale=scale),
                          reads=[psb[sb_]], writes=[PT_b[pt]])
                    for qq in range(2):
                        r, i = grp[q0 + qq]
                        blk0 = r * nj + i
                        ocol = (q0 + qq) * 128
                        for kk in range(2):
                            col = (qq * 2 + kk) * 128
                            P.add("pe", lambda e, nb_=nb_, ocol=ocol, ci=ci, blk=blk0 + kk, pt=pt, col=col, kk=kk: e.matmul(
                                ps[nb_][:, ocol:ocol + 128], lhsT=Vb[ci][:, blk, :], rhs=PT[pt][:, col:col + 128],
                                start=(kk == 0), stop=(kk == 1)),
                                reads=[Vb_b[ci], PT_b[pt]], writes=[psb[nb_]])
                        for kk in range(2):
                            col = (qq * 2 + kk) * 128
                            P.add("pe", lambda e, db_=db_, ocol=ocol, pt=pt, col=col, kk=kk: e.matmul(
                                ps[db_][:, ocol:ocol + 128], lhsT=C.ones_bf, rhs=PT[pt][:, col:col + 128],
                                start=(kk == 0), stop=(kk == 1)),
                                reads=[C.const_b, PT_b[pt]], writes=[psb[db_]])
                if d == 1:
                    i0 = grp[0][1]
                    views = [a[:, i0 * 128:i0 * 128 + 512] for a in (accn, accd)]
                    pviews = [ps[nb_], ps[db_]]
                elif d == 4:
                    r = grp[0][0]
                    views = [a.rearrange("p (i n r) -> p r i n", i=4, n=128, r=4)[:, r] for a in (accn, accd)]
                    pviews = [ps[b].rearrange("p (i n) -> p i n", i=4) for b in (nb_, db_)]
                else:
                    r0 = grp[0][0]
                    views = [a.rearrange("p (n r) -> p r n", r=16)[:, r0:r0 + 4, :] for a in (accn, accd)]
                    pviews = [ps[b].rearrange("p (i n) -> p i n", i=4) for b in (nb_, db_)]
                for (av, pv, ab, pb_) in ((views[0], pviews[0], accn_b, nb_), (views[1], pviews[1], accd_b, db_)):
                    if ci == 0:
                        P.add("act", lambda e, av=av, pv=pv: e.activation(out=av, in_=pv, func=AF.Copy),
                              reads=[psb[pb_]], writes=[ab])
                    else:
                        P.add("dve", lambda e, av=av, pv=pv: e.tensor_tensor(out=av, in0=av, in1=pv, op=ALU.add),
                              reads=[psb[pb_], ab], writes=[ab])
        P.add("dve", lambda e: e.reciprocal(out=accd, in_=accd), reads=[accd_b], writes=[accd_b])
        P.add("dve", lambda e: e.tensor_tensor(out=accn, in0=accn, in1=accd, op=ALU.mult), reads=[accn_b, accd_b], writes=[accn_b])
        emit_colnorm(P, C, accn, accn_b, S, rstd, rstd_b, sq, sq_b, [0, 1, 2, 3])
        P.add("dve", lambda e, hd=hd: e.scalar_tensor_tensor(out=ob, in0=accn, scalar=K_["gattn"][:, hd:hd + 1], in1=rstd,
                                                             op0=ALU.mult, op1=ALU.mult),
              reads=[accn_b, rstd_b, C.const_b], writes=[ob_b])
        P.add("sp", lambda e, hd=hd: e.dma_start(out=out_dram[hd], in_=ob), reads=[ob_b], writes=[out_b[hd]], dma="aout")
    P.barrier()
    A.reset(m0)


def load_consts(P, C, dram, names_dtypes):
    K_ = {}
    for name, (shape, dt) in names_dtypes.items():
        t = C.arena.alloc(shape, dt)
        eng = "pool" if dt == BF16 else "sp"
        P.add(eng, lambda e, t=t, name=name: e.dma_start(out=t, in_=dram[name]), writes=[C.const_b], dma="c_" + eng)
        K_[name] = t
    return K_


NFFT = 4096
NCH = 1024
NTC = SEQ // 128


def host_consts_hyena():
    c = {}
    L = SEQ
    t = np.linspace(0.0, 1.0, L, dtype=np.float32)
    bands = 16
    f = np.linspace(1e-4, bands - 1, bands, dtype=np.float32)[None, :]
    wpos = ((2.0 * np.pi) * np.arange(L, dtype=np.float32)[:, None] / L).astype(np.float32)
    emb = np.concatenate([t[:, None], np.cos(f * wpos), -np.sin(f * wpos)], axis=-1).astype(np.float32)
    c["embT"] = np.ascontiguousarray(emb.T)
    c["negt"] = np.ascontiguousarray((-t).reshape(NTC, 128).T)
    dm = np.ones((128, 2, NTC), np.float32)
    dm[0, 1, 0] = 0.0
    c["dirmask"] = dm
    a = np.arange(L, dtype=np.float64)
    ang0 = 2.0 * np.pi * np.outer(a, a + 0.5) / NFFT
    ang1 = 2.0 * np.pi * np.outer(a + 0.5, a + 0.5) / NFFT
    def tile(M):
        return np.ascontiguousarray(M.reshape(NTC, 128, NTC, 128).transpose(2, 1, 0, 3).astype(np.float32))
    c["C0t"] = tile(np.cos(ang0)); c["S0t"] = tile(np.sin(ang0))
    c["C1t"] = tile(np.cos(ang1)); c["S1t"] = tile(np.sin(ang1))
    return c


def sin_reduced(P, C, ps_ap, ps_b, freq, fb, out, out_b, tmp, tmp_b, np_):
    t, ti, tf, m = tmp
    I32 = mybir.dt.int32
    P.add("dve", lambda e: e.tensor_scalar(out=t, in0=ps_ap, scalar1=freq, scalar2=fb, op0=ALU.mult, op1=ALU.add),
          reads=[ps_b, C.const_b], writes=[tmp_b])
    P.add("dve", lambda e: e.tensor_scalar(out=t, in0=t, scalar1=1.0 / (2 * np.pi), scalar2=None, op0=ALU.mult),
          reads=[tmp_b], writes=[tmp_b])
    P.add("dve", lambda e: e.tensor_copy(out=ti.bitcast(I32), in_=t), reads=[tmp_b], writes=[tmp_b])
    P.add("dve", lambda e: e.tensor_copy(out=tf, in_=ti.bitcast(I32)), reads=[tmp_b], writes=[tmp_b])
    P.add("dve", lambda e: e.tensor_tensor(out=t, in0=t, in1=tf, op=ALU.subtract), reads=[tmp_b], writes=[tmp_b])
    P.add("dve", lambda e: e.tensor_single_scalar(out=m, in_=t, scalar=0.5, op=ALU.is_gt), reads=[tmp_b], writes=[tmp_b])
    P.add("dve", lambda e: e.tensor_tensor(out=t, in0=t, in1=m, op=ALU.subtract), reads=[tmp_b], writes=[tmp_b])
    P.add("dve", lambda e: e.tensor_single_scalar(out=m, in_=t, scalar=-0.5, op=ALU.is_lt), reads=[tmp_b], writes=[tmp_b])
    P.add("dve", lambda e: e.tensor_tensor(out=t, in0=t, in1=m, op=ALU.add), reads=[tmp_b], writes=[tmp_b])
    P.add("dve", lambda e: e.tensor_scalar(out=t, in0=t, scalar1=2 * np.pi, scalar2=None, op0=ALU.mult),
          reads=[tmp_b], writes=[tmp_b])
    P.add("dve", lambda e: e.tensor_scalar(out=t, in0=t, scalar1=-3.14159, scalar2=3.14159, op0=ALU.max, op1=ALU.min),
          reads=[tmp_b], writes=[tmp_b])
    P.add("act", lambda e: e.activation(out=out, in_=t, func=AF.Sin), reads=[tmp_b], writes=[out_b])


class WSlots:
    def __init__(self, P, C, n, elems, tag):
        self.P = P
        self.t = [C.arena.alloc([elems], BF16) for _ in range(n)]
        self.b = [Buf() for _ in range(n)]
        self.n = n
        self.ctr = 0
        self.tag = tag

    def load(self, src_ap, nel):
        s = self.ctr % self.n
        self.ctr += 1
        view = self.t[s][:, 0:nel]
        self.P.add("pool", lambda e: e.dma_start(out=view, in_=src_ap), writes=[self.b[s]], dma=f"{self.tag}{s}")
        return self.t[s], self.b[s]


def emit_filters(P, C, D_, Kd, Kd_b):
    A = C.arena
    m0 = A.mark()
    ps, psb = C.psum, C.psum_b
    S = SEQ
    spec = {"embT": ([S], F32), "w1": ([64], F32), "fvec": ([4], F32), "w2": ([64], F32), "w3": ([4 * NCH], F32),
            "negt": ([NTC], F32), "dirmask": ([2, NTC], F32), "delta_b": ([NCH], F32)}
    K_ = {}
    for name, (shape, dt) in spec.items():
        t = A.alloc(shape, dt)
        np_ = D_[name].shape[0]
        P.add("sp", lambda e, t=t, name=name, np_=np_: e.dma_start(out=t[0:np_], in_=D_[name]), writes=[C.const_b], dma="c_sp")
        K_[name] = t
    fb = A.alloc([2], F32)
    P.add("dve", lambda e: e.tensor_tensor(out=fb[0:64, 0:1], in0=K_["fvec"][0:64, 0:1], in1=K_["fvec"][0:64, 1:2], op=ALU.mult),
          reads=[C.const_b], writes=[C.const_b])
    P.add("dve", lambda e: e.tensor_tensor(out=fb[0:64, 1:2], in0=K_["fvec"][0:64, 2:3], in1=K_["fvec"][0:64, 3:4], op=ALU.mult),
          reads=[C.const_b], writes=[C.const_b])
    h1 = A.alloc([S], F32); h1_b = Buf()
    h2 = A.alloc([S], F32); h2_b = Buf()
    tmp4 = [A.alloc([512], F32)[0:64, :] for _ in range(4)]; tmp_b = Buf()
    for h in range(4):
        P.add("pe", lambda e, h=h: e.matmul(ps[h][0:64, :], lhsT=K_["w1"][0:33, :], rhs=K_["embT"][0:33, h * 512:(h + 1) * 512],
                                            start=True, stop=True), reads=[C.const_b], writes=[psb[h]])
        sin_reduced(P, C, ps[h][0:64, :], psb[h], K_["fvec"][0:64, 0:1], fb[0:64, 0:1], h1[0:64, h * 512:(h + 1) * 512], h1_b,
                    tmp4, tmp_b, 64)
    for h in range(4):
        P.add("pe", lambda e, h=h: e.matmul(ps[4 + h][0:64, :], lhsT=K_["w2"][0:64, :], rhs=h1[0:64, h * 512:(h + 1) * 512],
                                            start=True, stop=True), reads=[C.const_b, h1_b], writes=[psb[4 + h]])
        sin_reduced(P, C, ps[4 + h][0:64, :], psb[4 + h], K_["fvec"][0:64, 2:3], fb[0:64, 1:2], h2[0:64, h * 512:(h + 1) * 512], h2_b,
                    tmp4, tmp_b, 64)
    hsum = A.alloc([NTC, NCH], BF16); hsum_b = Buf()
    hdif = A.alloc([NTC, NCH], BF16); hdif_b = Buf()
    dec = A.alloc([NCH], F32); dec_b = Buf()
    ff = [A.alloc([NCH], F32) for _ in range(2)]
    ff_b = [Buf() for _ in range(2)]
    ws = WSlots(P, C, 4, NTC * 128, "wf")
    ko = [A.alloc([NCH], F32) for _ in range(4)]
    ko_b = [Buf() for _ in range(4)]
    cnt = 0
    for n in range(2):
        for tc in range(NTC):
            P.add("act", lambda e, tc=tc: e.activation(out=dec, in_=K_["delta_b"], func=AF.Exp, scale=K_["negt"][:, tc:tc + 1]),
                  reads=[C.const_b], writes=[dec_b])
            for dr in range(2):
                grp = n * 2 + dr
                for hf in range(2):
                    b = (cnt % 4) * 2 + hf if False else (cnt % 2) * 4 + dr * 2 + hf
                    P.add("pe", lambda e, b=b, tc=tc, grp=grp, hf=hf: e.matmul(
                        ps[b], lhsT=h2[0:64, tc * 128:(tc + 1) * 128], rhs=K_["w3"][0:64, grp * NCH + hf * 512: grp * NCH + (hf + 1) * 512],
                        start=True, stop=True), reads=[h2_b, C.const_b], writes=[psb[b]])
                    P.add("dve", lambda e, b=b, dr=dr, tc=tc, hf=hf: e.scalar_tensor_tensor(
                        out=ff[dr][:, hf * 512:(hf + 1) * 512], in0=ps[b], scalar=K_["dirmask"][:, dr, tc:tc + 1],
                        in1=dec[:, hf * 512:(hf + 1) * 512], op0=ALU.mult, op1=ALU.mult),
                        reads=[psb[b], dec_b, C.const_b], writes=[ff_b[dr]])
            cnt += 1
            P.add("dve", lambda e, tc=tc: e.tensor_tensor(out=hsum[:, tc, :], in0=ff[0], in1=ff[1], op=ALU.add),
                  reads=[ff_b[0], ff_b[1]], writes=[hsum_b])
            P.add("dve", lambda e, tc=tc: e.tensor_tensor(out=hdif[:, tc, :], in0=ff[0], in1=ff[1], op=ALU.subtract),
                  reads=[ff_b[0], ff_b[1]], writes=[hdif_b])
        for fc in range(NTC):
            ct, ct_b = ws.load(D_["C0t"][fc].rearrange("p a b -> p (a b)"), NTC * 128)
            st_, st_b = ws.load(D_["S0t"][fc].rearrange("p a b -> p (a b)"), NTC * 128)
            pb = (fc % 2) * 4
            for (wt, wt_b, src, src_b, boff) in ((ct, ct_b, hsum, hsum_b, 0), (st_, st_b, hdif, hdif_b, 2)):
                wv = wt[:, 0:NTC * 128].rearrange("p (a b) -> p a b", a=NTC)
                for hf in range(2):
                    for tc in range(NTC):
                        P.add("pe", lambda e, wv=wv, tc=tc, hf=hf, src=src, b=pb + boff + hf: e.matmul(
                            ps[b], lhsT=wv[:, tc, :], rhs=src[:, tc, hf * 512:(hf + 1) * 512], start=(tc == 0), stop=(tc == NTC - 1)),
                            reads=[wt_b, src_b], writes=[psb[pb + boff + hf]])
            for ri, sc in ((0, 2.0 / NFFT), (1, -2.0 / NFFT)):
                k_ = (fc % 2) * 2 + ri
                for hf in range(2):
                    b = pb + ri * 2 + hf
                    P.add("act", lambda e, k_=k_, hf=hf, b=b, sc=sc: e.activation(out=ko[k_][:, hf * 512:(hf + 1) * 512], in_=ps[b],
                                                                                 func=AF.Copy, scale=sc),
                          reads=[psb[b]], writes=[ko_b[k_]])
                P.add("sp", lambda e, n=n, ri=ri, fc=fc, k_=k_: e.dma_start(out=Kd[n, ri, fc], in_=ko[k_]),
                      reads=[ko_b[k_]], writes=[Kd_b], dma=f"ko{k_}")
    P.barrier()
    A.reset(m0)


def emit_hyena_prep(P, C, pj, pj_b, c0, cw_d, ident_f, U, U_b):
    A = C.arena
    m0 = A.mark()
    ps, psb = C.psum, C.psum_b
    S = SEQ
    cw = A.alloc([24, 4], F32)
    P.add("sp", lambda e: e.dma_start(out=cw, in_=cw_d), writes=[C.const_b], dma="c_sp")
    u = [A.alloc([S], F32) for _ in range(2)]
    u_b = [Buf() for _ in range(2)]
    acc = [A.alloc([S], F32) for _ in range(2)]
    acc_b = [Buf() for _ in range(2)]
    stg = [A.alloc([NTC, 128], F32) for _ in range(2)]
    stg_b = [Buf() for _ in range(2)]
    for c in range(24):
        s = c % 2
        w, cc = c // 8, c % 8
        P.add("sp", lambda e, s=s, c=c: e.dma_start(out=u[s], in_=pj[c0 + c]), reads=[pj_b[c0 + c]], writes=[u_b[s]], dma=f"hu{s}")
        P.add("dve", lambda e, s=s, c=c: e.tensor_scalar(out=acc[s], in0=u[s], scalar1=cw[:, c, 1:2], scalar2=cw[:, c, 3:4],
                                                         op0=ALU.mult, op1=ALU.add), reads=[u_b[s], C.const_b], writes=[acc_b[s]])
        P.add("dve", lambda e, s=s, c=c: e.scalar_tensor_tensor(out=acc[s][:, 1:S], in0=u[s][:, 0:S - 1], scalar=cw[:, c, 0:1],
                                                                in1=acc[s][:, 1:S], op0=ALU.mult, op1=ALU.add),
              reads=[u_b[s], acc_b[s], C.const_b], writes=[acc_b[s]])
        P.add("dve", lambda e, s=s, c=c: e.scalar_tensor_tensor(out=acc[s][:, 0:S - 1], in0=u[s][:, 1:S], scalar=cw[:, c, 2:3],
                                                                in1=acc[s][:, 0:S - 1], op0=ALU.mult, op1=ALU.add),
              reads=[u_b[s], acc_b[s], C.const_b], writes=[acc_b[s]])
        for g in range(4):
            bank = (c * 4 + g) % 8
            for k in range(4):
                tc = g * 4 + k
                P.add("pe", lambda e, bank=bank, k=k, tc=tc, s=s: e.matmul(ps[bank][:, k * 128:(k + 1) * 128],
                                                                          lhsT=acc[s][:, tc * 128:(tc + 1) * 128], rhs=ident_f,
                                                                          start=True, stop=True),
                      reads=[acc_b[s], C.const_b], writes=[psb[bank]])
            P.add("act", lambda e, bank=bank, g=g, s=s: e.activation(out=stg[s][:, g * 4:(g + 1) * 4, :],
                                                                    in_=ps[bank].rearrange("p (a b) -> p a b", a=4), func=AF.Copy),
                  reads=[psb[bank]], writes=[stg_b[s]])
        P.add("sp", lambda e, s=s, w=w, cc=cc: e.dma_start(out=U[w][:, :, cc * 128:(cc + 1) * 128].rearrange("t p j -> p t j"), in_=stg[s]),
              reads=[stg_b[s]], writes=[U_b[w]], dma=f"hs{s}")
    P.barrier()
    A.reset(m0)


def emit_hyena_order(P, C, n, zsrc, zsrc_b, gate, gate_b, Kd, Kd_b, bias_d, C1t, S1t, zdst, zdst_b, final=None):
    A = C.arena
    m0 = A.mark()
    ps, psb = C.psum, C.psum_b
    ztm = A.alloc([NTC, NCH], BF16); ztm_b = Buf()
    Yr = A.alloc([NTC, NCH], BF16); Yr_b = Buf()
    Yn = A.alloc([NTC, NCH], BF16); Yn_b = Buf()
    ws = WSlots(P, C, 4, NTC * 128, f"wh{n}")
    P.add("pool", lambda e: e.dma_start(out=ztm, in_=zsrc.rearrange("t p c -> p t c")), reads=[zsrc_b], writes=[ztm_b], dma=f"hz{n}")
    m1 = A.mark()
    kt = [[A.alloc([NCH], F32) for _ in range(2)] for _ in range(2)]
    kt_b = [[Buf() for _ in range(2)] for _ in range(2)]
    tt = [A.alloc([NCH], F32) for _ in range(4)]
    tt_b = [Buf() for _ in range(4)]
    for fc in range(NTC):
        ct, ct_b = ws.load(C1t[fc].rearrange("p a b -> p (a b)"), NTC * 128)
        st_, st_b = ws.load(S1t[fc].rearrange("p a b -> p (a b)"), NTC * 128)
        s = fc % 2
        pb = s * 4
        for ri in range(2):
            P.add("sp", lambda e, s=s, ri=ri, fc=fc: e.dma_start(out=kt[s][ri], in_=Kd[n, ri, fc]),
                  reads=[Kd_b], writes=[kt_b[s][ri]], dma=f"hk{s}{ri}")
        for (wt, wt_b, boff) in ((ct, ct_b, 0), (st_, st_b, 2)):
            wv = wt[:, 0:NTC * 128].rearrange("p (a b) -> p a b", a=NTC)
            for hf in range(2):
                for tc in range(NTC):
                    P.add("pe", lambda e, wv=wv, tc=tc, hf=hf, b=pb + boff + hf: e.matmul(
                        ps[b], lhsT=wv[:, tc, :], rhs=ztm[:, tc, hf * 512:(hf + 1) * 512], start=(tc == 0), stop=(tc == NTC - 1)),
                        reads=[wt_b, ztm_b], writes=[psb[pb + boff + hf]])
        for hf in range(2):
            sl = slice(hf * 512, (hf + 1) * 512)
            zc, zs = ps[pb + hf], ps[pb + 2 + hf]
            zc_b, zs_b = psb[pb + hf], psb[pb + 2 + hf]
            P.add("dve", lambda e, sl=sl, zc=zc, s=s: e.tensor_tensor(out=tt[0][:, sl], in0=zc, in1=kt[s][0][:, sl], op=ALU.mult),
                  reads=[zc_b, kt_b[s][0]], writes=[tt_b[0]])
            P.add("dve", lambda e, sl=sl, zs=zs, s=s: e.tensor_tensor(out=tt[1][:, sl], in0=zs, in1=kt[s][1][:, sl], op=ALU.mult),
                  reads=[zs_b, kt_b[s][1]], writes=[tt_b[1]])
            P.add("dve", lambda e, sl=sl, zs=zs, s=s: e.tensor_tensor(out=tt[2][:, sl], in0=zs, in1=kt[s][0][:, sl], op=ALU.mult),
                  reads=[zs_b, kt_b[s][0]], writes=[tt_b[2]])
            P.add("dve", lambda e, sl=sl, zc=zc, s=s: e.tensor_tensor(out=tt[3][:, sl], in0=zc, in1=kt[s][1][:, sl], op=ALU.mult),
                  reads=[zc_b, kt_b[s][1]], writes=[tt_b[3]])
        P.add("dve", lambda e, fc=fc: e.tensor_tensor(out=Yr[:, fc, :], in0=tt[0], in1=tt[1], op=ALU.add),
              reads=[tt_b[0], tt_b[1]], writes=[Yr_b])
        P.add("dve", lambda e, fc=fc: e.tensor_tensor(out=Yn[:, fc, :], in0=tt[2], in1=tt[3], op=ALU.subtract),
              reads=[tt_b[2], tt_b[3]], writes=[Yn_b])
    P.barrier()
    A.reset(m1)
    bias_b = A.alloc([NCH], F32)
    P.add("sp", lambda e: e.dma_start(out=bias_b, in_=bias_d), writes=[C.const_b], dma="c_sp")
    zf = [A.alloc([NCH], F32) for _ in range(2)]; zf_b = [Buf() for _ in range(2)]
    gt = [A.alloc([NCH], F32) for _ in range(2)]; gt_b = [Buf() for _ in range(2)]
    zo = [A.alloc([NCH], F32) for _ in range(2)]; zo_b = [Buf() for _ in range(2)]
    if final is not None:
        ghy = A.alloc([8], F32)
        P.add("sp", lambda e: e.dma_start(out=ghy, in_=final["ghy_d"]), writes=[C.const_b], dma="c_sp")
        sqt = A.alloc([NCH], F32); sqt_b = Buf()
        ss = A.alloc([8], F32); ss_b = Buf()
        mT = A.alloc([8, SEQ], BF16); mT_b = Buf()
    for tc in range(NTC):
        ct, ct_b = ws.load(C1t[tc].rearrange("p a b -> p (a b)"), NTC * 128)
        st_, st_b = ws.load(S1t[tc].rearrange("p a b -> p (a b)"), NTC * 128)
        s = tc % 2
        pb = s * 2
        P.add("sp", lambda e, s=s, tc=tc: e.dma_start(out=zf[s], in_=zsrc[tc]), reads=[zsrc_b], writes=[zf_b[s]], dma=f"hzf{s}")
        P.add("sp", lambda e, s=s, tc=tc: e.dma_start(out=gt[s], in_=gate[tc]), reads=[gate_b], writes=[gt_b[s]], dma=f"hgt{s}")
        cv = ct[:, 0:NTC * 128].rearrange("p (a b) -> p a b", a=NTC)
        sv = st_[:, 0:NTC * 128].rearrange("p (a b) -> p a b", a=NTC)
        for hf in range(2):
            for fc in range(NTC):
                P.add("pe", lambda e, cv=cv, fc=fc, hf=hf, b=pb + hf: e.matmul(
                    ps[b], lhsT=cv[:, fc, :], rhs=Yr[:, fc, hf * 512:(hf + 1) * 512], start=(fc == 0), stop=False),
                    reads=[ct_b, Yr_b], writes=[psb[pb + hf]])
            for fc in range(NTC):
                P.add("pe", lambda e, sv=sv, fc=fc, hf=hf, b=pb + hf: e.matmul(
                    ps[b], lhsT=sv[:, fc, :], rhs=Yn[:, fc, hf * 512:(hf + 1) * 512], start=False, stop=(fc == NTC - 1)),
                    reads=[st_b, Yn_b], writes=[psb[pb + hf]])
        P.add("dve", lambda e, s=s: e.tensor_tensor(out=zo[s], in0=zf[s], in1=bias_b, op=ALU.mult),
              reads=[zf_b[s], C.const_b, zo_b[s]], writes=[zo_b[s]])
        for hf in range(2):
            sl = slice(hf * 512, (hf + 1) * 512)
            P.add("dve", lambda e, s=s, sl=sl, b=pb + hf: e.tensor_tensor(out=zo[s][:, sl], in0=zo[s][:, sl], in1=ps[b], op=ALU.add),
                  reads=[psb[pb + hf], zo_b[s]], writes=[zo_b[s]])
        P.add("dve", lambda e, s=s: e.tensor_tensor(out=zo[s], in0=zo[s], in1=gt[s], op=ALU.mult),
              reads=[gt_b[s], zo_b[s]], writes=[zo_b[s]])
        if final is None:
            P.add("sp", lambda e, s=s, tc=tc: e.dma_start(out=zdst[tc], in_=zo[s]), reads=[zo_b[s]], writes=[zdst_b], dma=f"hzo{s}")
        else:
            ident_f = final["ident_f"]
            P.add("dve", lambda e, s=s: e.tensor_tensor(out=sqt, in0=zo[s], in1=zo[s], op=ALU.mult), reads=[zo_b[s]], writes=[sqt_b])
            P.add("dve", lambda e: e.tensor_reduce(out=ss, in_=sqt.rearrange("p (g c) -> p g c", g=8), axis=AX.X, op=ALU.add),
                  reads=[sqt_b], writes=[ss_b])
            P.add("dve", lambda e: e.tensor_scalar(out=ss, in0=ss, scalar1=1.0 / 128, scalar2=RMS_EPS, op0=ALU.mult, op1=ALU.add),
                  reads=[ss_b], writes=[ss_b])
            P.add("act", lambda e: e.activation(out=ss, in_=ss, func=AF.Sqrt), reads=[ss_b], writes=[ss_b])
            P.add("dve", lambda e: e.reciprocal(out=ss, in_=ss), reads=[ss_b], writes=[ss_b])
            for g in range(8):
                P.add("dve", lambda e, s=s, g=g: e.tensor_scalar(out=zo[s][:, g * 128:(g + 1) * 128], in0=zo[s][:, g * 128:(g + 1) * 128],
                                                                 scalar1=ss[:, g:g + 1], scalar2=None, op0=ALU.mult),
                      reads=[ss_b, zo_b[s]], writes=[zo_b[s]])
            for g4 in range(2):
                bank = 4 + (tc * 2 + g4) % 4
                for k in range(4):
                    g = g4 * 4 + k
                    P.add("pe", lambda e, bank=bank, k=k, g=g, s=s: e.matmul(ps[bank][:, k * 128:(k + 1) * 128],
                                                                            lhsT=zo[s][:, g * 128:(g + 1) * 128], rhs=ident_f,
                                                                            start=True, stop=True),
                          reads=[zo_b[s], C.const_b], writes=[psb[bank]])
                for k in range(4):
                    g = g4 * 4 + k
                    P.add("dve", lambda e, bank=bank, k=k, g=g, tc=tc: e.tensor_scalar(
                        out=mT[:, g, tc * 128:(tc + 1) * 128], in0=ps[bank][:, k * 128:(k + 1) * 128], scalar1=ghy[:, g:g + 1],
                        scalar2=None, op0=ALU.mult), reads=[psb[bank], C.const_b], writes=[mT_b])
    if final is not None:
        for g in range(8):
            P.add("sp", lambda e, g=g: e.dma_start(out=final["out"][g], in_=mT[:, g, :]), reads=[mT_b], writes=[final["out_b"][g]],
                  dma="hout")
    P.barrier()
    A.reset(m0)


def emit_hyena(P, C, nc, pj, pj_b, c0, HD, out, out_b, tag=""):
    Kd = nc.dram_tensor("Kd" + tag, [2, 2, NTC, 128, NCH], F32, kind="Internal").ap()
    U = nc.dram_tensor("Utm" + tag, [3, NTC, 128, NCH], F32, kind="Internal").ap()
    Z1 = nc.dram_tensor("Z1" + tag, [NTC, 128, NCH], F32, kind="Internal").ap()
    Kd_b = Buf(); U_b = [Buf() for _ in range(3)]; Z1_b = Buf()
    identf = C.arena.alloc([128], F32)
    P.add("sp", lambda e: e.dma_start(out=identf, in_=HD["ident"]), writes=[C.const_b], dma="c_sp")
    emit_filters(P, C, HD, Kd, Kd_b)
    emit_hyena_prep(P, C, pj, pj_b, c0, HD["cw"], identf, U, U_b)
    emit_hyena_order(P, C, 0, U[0], U_b[0], U[1], U_b[1], Kd, Kd_b, HD["bias0"], HD["C1t"], HD["S1t"], Z1, Z1_b)
    emit_hyena_order(P, C, 1, Z1, Z1_b, U[2], U_b[2], Kd, Kd_b, HD["bias1"], HD["C1t"], HD["S1t"], None, None,
                     final=dict(ghy_d=HD["ghy"], ident_f=identf, out=out, out_b=out_b))


def emit_resid_proj(P, C, act, act_b, nin, w_t, x_src, x_dst, xbufs, scale, D, NT, tag):
    A = C.arena
    m0 = A.mark()
    ps, psb = C.psum, C.psum_b
    KC = D // 128
    NH = NT // 512
    ws = WSlots(P, C, 4, nin * 128, "wr" + tag)
    xr = [A.alloc([NT], F32) for _ in range(2)]; xr_b = [Buf() for _ in range(2)]
    xo = [A.alloc([NT], F32) for _ in range(2)]; xo_b = [Buf() for _ in range(2)]
    for dc in range(KC):
        wt, wt_b = ws.load(w_t[dc].rearrange("p k f -> p (k f)"), nin * 128)
        wv = wt[:, 0:nin * 128].rearrange("p (k f) -> p k f", k=nin)
        s2 = dc % 2
        pb = s2 * 4
        P.add("sp", lambda e, s2=s2, dc=dc: e.dma_start(out=xr[s2], in_=x_src[dc * 128:(dc + 1) * 128, :]),
              reads=[xbufs[dc]], writes=[xr_b[s2]], dma=f"rx{tag}{s2}")
        for h in range(NH):
            for k in range(nin):
                P.add("pe", lambda e, wv=wv, k=k, h=h, bank=pb + h: e.matmul(
                    ps[bank], lhsT=wv[:, k, :], rhs=act[:, k, h * 512:(h + 1) * 512], start=(k == 0), stop=(k == nin - 1)),
                    reads=[wt_b, act_b], writes=[psb[pb + h]])
        for h in range(NH):
            P.add("dve", lambda e, s2=s2, h=h, bank=pb + h: e.scalar_tensor_tensor(
                out=xo[s2][:, h * 512:(h + 1) * 512], in0=ps[bank], scalar=scale, in1=xr[s2][:, h * 512:(h + 1) * 512],
                op0=ALU.mult, op1=ALU.add), reads=[psb[pb + h], xr_b[s2]], writes=[xo_b[s2]])
        P.add("sp", lambda e, s2=s2, dc=dc: e.dma_start(out=x_dst[dc * 128:(dc + 1) * 128, :], in_=xo[s2]),
              reads=[xo_b[s2]], writes=[xbufs[dc]], dma=f"ro{tag}{s2}")
    P.barrier()
    A.reset(m0)


def emit_proj_fm(P, C, hT, hT_b, w_t, nchunks, pj, pj_b, D, NTOK):
    A = C.arena
    m0 = A.mark()
    ps, psb = C.psum, C.psum_b
    KC = D // 128
    NH = NTOK // 512
    ws = WSlots(P, C, 4, KC * 128, "wp")
    stg = [A.alloc([NTOK], F32) for _ in range(2)]; stg_b = [Buf() for _ in range(2)]
    for c in range(nchunks):
        wt, wt_b = ws.load(w_t[c].rearrange("p k f -> p (k f)"), KC * 128)
        wv = wt[:, 0:KC * 128].rearrange("p (k f) -> p k f", k=KC)
        s2 = c % 2
        pb = (s2 * NH) % 8
        for h in range(NH):
            for k in range(KC):
                P.add("pe", lambda e, wv=wv, k=k, h=h, bank=(pb + h) % 8: e.matmul(
                    ps[bank], lhsT=wv[:, k, :], rhs=hT[:, k, h * 512:(h + 1) * 512], start=(k == 0), stop=(k == KC - 1)),
                    reads=[wt_b, hT_b], writes=[psb[(pb + h) % 8]])
        for h in range(NH):
            eng = "act" if h % 2 == 0 else "dve"
            if eng == "act":
                P.add("act", lambda e, s2=s2, h=h, bank=(pb + h) % 8: e.activation(out=stg[s2][:, h * 512:(h + 1) * 512], in_=ps[bank], func=AF.Copy),
                      reads=[psb[(pb + h) % 8]], writes=[stg_b[s2]])
            else:
                P.add("dve", lambda e, s2=s2, h=h, bank=(pb + h) % 8: e.tensor_copy(out=stg[s2][:, h * 512:(h + 1) * 512], in_=ps[bank]),
                      reads=[psb[(pb + h) % 8]], writes=[stg_b[s2]])
        P.add("sp", lambda e, s2=s2, c=c: e.dma_start(out=pj[c], in_=stg[s2]), reads=[stg_b[s2]], writes=[pj_b[c]], dma=f"pjo{s2}")
    P.barrier()
    A.reset(m0)


D_MODEL = 4096
FFN_HIDDEN = 11008
NT_CORE = 1024
ATT_SPEC = {"cosT": ([SEQ], F32), "sinT": ([SEQ], F32), "rotm": ([128], F32), "ident_b": ([128], BF16),
            "maskb": ([4, 256], BF16), "gq": ([1], F32), "gk": ([1], F32), "gattn": ([8], F32)}
ATT_DSHAPE = {"cosT": [128, SEQ], "sinT": [128, SEQ], "rotm": [128, 128], "ident_b": [128, 128], "maskb": [128, 4, 256],
              "gq": [128, 1], "gk": [128, 1], "gattn": [128, 8]}
HY_DSHAPE = {"embT": [33, SEQ], "w1": [33, 64], "fvec": [64, 4], "w2": [64, 64], "w3": [64, 4 * NCH], "negt": [128, NTC],
             "dirmask": [128, 2, NTC], "delta_b": [128, NCH], "C0t": [NTC, 128, NTC, 128], "S0t": [NTC, 128, NTC, 128],
             "C1t": [NTC, 128, NTC, 128], "S1t": [NTC, 128, NTC, 128], "ident": [128, 128], "cw": [128, 24, 4],
             "bias0": [128, NCH], "bias1": [128, NCH], "ghy": [128, 8]}


def _new_nc():
    return bass.Bass("TRN2", target_bir_lowering=False)


def build_m1_prog():
    nc = _new_nc()
    D, NT, KC = D_MODEL, NT_CORE, D_MODEL // 128
    x = nc.dram_tensor("x", [D, NT], F32, kind="ExternalInput").ap()
    g = nc.dram_tensor("g", [128, KC], F32, kind="ExternalInput").ap()
    hout = nc.dram_tensor("h", [KC, 128, NT], BF16, kind="ExternalOutput").ap()
    with contextlib.ExitStack() as st:
        st.enter_context(nc.allow_low_precision("bf16 matmul operands (problem spec)"))
        P = Prog(nc)
        C = make_ctx(nc, st, P)
        g_sb = C.arena.alloc([KC], F32)
        P.add("sp", lambda e: e.dma_start(out=g_sb, in_=g), writes=[C.const_b], dma="c_sp")
        xbufs = [Buf() for _ in range(KC)]
        hT = C.arena.alloc([KC, NT], BF16); hT_b = Buf()
        emit_rmsnorm_hT(P, C, x, xbufs, g_sb, hT, hT_b, D, NT)
        ob = Buf()
        P.add("sp", lambda e: e.dma_start(out=hout.rearrange("k p t -> p k t"), in_=hT), reads=[hT_b], writes=[ob], dma="hout")
        P.barrier()
        P.emit()
    return nc


def emit_m2(P, C, nc, hfull, hfull_b, w_t, AD, HD, out, out_b, tag=""):
    KC = D_MODEL // 128
    A = C.arena
    m0 = A.mark()
    pj = nc.dram_tensor("pj" + tag, [48, 128, SEQ], F32, kind="Internal").ap()
    pj_b = [Buf() for _ in range(48)]
    hT = A.alloc([KC, SEQ], BF16); hT_b = Buf()
    for q4 in range(4):
        P.add("sp", lambda e, q4=q4: e.dma_start(out=hT[:, q4 * 8:(q4 + 1) * 8, :], in_=hfull[q4 * 8:(q4 + 1) * 8].rearrange("k p t -> p k t")),
              reads=[hfull_b], writes=[hT_b], dma="hin")
    emit_proj_fm(P, C, hT, hT_b, w_t, 48, pj, pj_b, D_MODEL, SEQ)
    A.reset(m0)
    m1 = A.mark()
    K_ = load_consts(P, C, AD, ATT_SPEC)
    emit_attention(P, C, pj, pj_b, 8, K_, out[0:8], out_b[0:8])
    A.reset(m1)
    emit_hyena(P, C, nc, pj, pj_b, 24, HD, out[8:16], out_b[8:16], tag=tag)
    A.reset(m0)


def build_m2_prog():
    nc = _new_nc()
    KC = D_MODEL // 128
    hfull = nc.dram_tensor("hfull", [KC, 128, SEQ], BF16, kind="ExternalInput").ap()
    w_t = nc.dram_tensor("w_in", [48, 128, KC, 128], F32, kind="ExternalInput").ap()
    AD = {k: nc.dram_tensor("a_" + k, v, F32, kind="ExternalInput").ap() for k, v in ATT_DSHAPE.items()}
    HD = {k: nc.dram_tensor("h_" + k, v, F32, kind="ExternalInput").ap() for k, v in HY_DSHAPE.items()}
    out = nc.dram_tensor("merged", [16, 128, SEQ], BF16, kind="ExternalOutput").ap()
    with contextlib.ExitStack() as st:
        st.enter_context(nc.allow_low_precision("bf16 matmul operands (problem spec)"))
        P = Prog(nc)
        C = make_ctx(nc, st, P)
        emit_m2(P, C, nc, hfull, Buf(), w_t, AD, HD, out, [Buf() for _ in range(16)])
        P.emit()
    return nc


def build_m3_prog():
    nc = _new_nc()
    D, NT, KC = D_MODEL, NT_CORE, D_MODEL // 128
    x = nc.dram_tensor("x", [D, NT], F32, kind="ExternalInput").ap()
    mg = nc.dram_tensor("mg", [KC, 128, NT], BF16, kind="ExternalInput").ap()
    w_t = nc.dram_tensor("w_out", [KC, 128, KC, 128], F32, kind="ExternalInput").ap()
    y = nc.dram_tensor("y", [D, NT], F32, kind="ExternalOutput").ap()
    with contextlib.ExitStack() as st:
        st.enter_context(nc.allow_low_precision("bf16 matmul operands (problem spec)"))
        P = Prog(nc)
        C = make_ctx(nc, st, P)
        act = C.arena.alloc([KC, NT], BF16); act_b = Buf()
        P.add("sp", lambda e: e.dma_start(out=act, in_=mg.rearrange("k p t -> p k t")), writes=[act_b], dma="min")
        xbufs = [Buf() for _ in range(KC)]
        emit_resid_proj(P, C, act, act_b, KC, w_t, x, y, xbufs, 1.0, D, NT, "o")
        P.emit()
    return nc


def tile_w_out(w, Din, Dout):
    return np.ascontiguousarray(w.reshape(Din // 128, 128, Dout // 128, 128).transpose(2, 1, 0, 3))


def hyena_host_inputs(s, conv_w, conv_b, w1, b1, w2, b2, freq, w3, fbias, gh, hc):
    Cw = 2048
    d = {}
    for k in ("embT", "negt", "dirmask", "C0t", "S0t", "C1t", "S1t"):
        d[k] = hc[k]
    d["w1"] = np.ascontiguousarray(w1); d["w2"] = np.ascontiguousarray(w2)
    d["fvec"] = np.ascontiguousarray(np.stack([freq[0], b1, freq[1], b2], 1).astype(np.float32))
    ch = slice(s * NCH, (s + 1) * NCH)
    d["w3"] = np.ascontiguousarray(w3.reshape(64, 4, Cw)[:, :, ch].reshape(64, 4 * NCH))
    d["delta_b"] = np.ascontiguousarray(np.broadcast_to(hc["deltas"][ch][None, :], (128, NCH)))
    d["ident"] = hc["ident"]
    cw = np.zeros((128, 24, 4), np.float32)
    for c in range(24):
        w, cc = c // 8, c % 8
        chan = w * Cw + s * NCH + cc * 128 + np.arange(128)
        cw[:, c, 0:3] = conv_w[:, chan].T
        cw[:, c, 3] = conv_b[chan]
    d["cw"] = cw
    d["bias0"] = np.ascontiguousarray(np.broadcast_to(fbias[0, ch][None, :], (128, NCH)))
    d["bias1"] = np.ascontiguousarray(np.broadcast_to(fbias[1, ch][None, :], (128, NCH)))
    d["ghy"] = np.ascontiguousarray(gh[ch].reshape(8, 128).T)
    return d


def all_host_consts():
    hc = host_consts()
    hc.update(host_consts_hyena())
    hc["deltas"] = np.abs(np.linspace(np.log(0.3) / 1e-2, np.log(1.5) / 1e-2, 2048, dtype=np.float32)).astype(np.float32)
    return hc


def w_in_chunk_ids(s):
    ids = []
    for blk in range(3):
        ids += [blk * 16 + s * 8 + h for h in range(8)]
    for w in range(3):
        ids += [48 + w * 16 + s * 8 + c for c in range(8)]
    return ids


_PROGS = {}


def _prog(name, fn):
    if name not in _PROGS:
        _PROGS[name] = fn()
    return _PROGS[name]


def kernel(x, ffn_norm, ffn_w_gate, ffn_w_up, ffn_w_down, mix_norm, w_in, q_norm, k_norm,
           conv_w, conv_b, filt_w1, filt_b1, filt_w2, filt_b2, filt_freq, filt_w3, filt_bias,
           attn_out_norm, hyena_out_norm, w_out):
    import ml_dtypes
    f32 = np.float32
    args = [np.asarray(a, dtype=f32) for a in (x, ffn_norm, ffn_w_gate, ffn_w_up, ffn_w_down, mix_norm, w_in, q_norm, k_norm,
                                               conv_w, conv_b, filt_w1, filt_b1, filt_w2, filt_b2, filt_freq, filt_w3, filt_bias,
                                               attn_out_norm, hyena_out_norm, w_out)]
    (x, ffn_norm, ffn_w_gate, ffn_w_up, ffn_w_down, mix_norm, w_in, q_norm, k_norm,
     conv_w, conv_b, filt_w1, filt_b1, filt_w2, filt_b2, filt_freq, filt_w3, filt_bias,
     attn_out_norm, hyena_out_norm, w_out) = args
    B, S, D = x.shape
    KC = D // 128
    cores = list(range(8))
    hc = all_host_consts()
    xT = [np.ascontiguousarray(x[c // 2, (c % 2) * 1024:(c % 2 + 1) * 1024, :].T) for c in cores]
    colvec = lambda g: np.ascontiguousarray(g.reshape(-1, 128).T)

    def run_ffn(xT, l, j):
        nc = _prog("ffn", lambda: build_ffn_prog(D_MODEL, FFN_HIDDEN, NT_CORE, 3))
        wg = tile_w_in(ffn_w_gate[l, j], D)
        wu = tile_w_in(ffn_w_up[l, j], D)
        wd = tile_w_out(ffn_w_down[l, j], FFN_HIDDEN, D)
        g = colvec(ffn_norm[l, j])
        res = run_bass_kernel_spmd(nc, [{"x": xT[c], "g": g, "wg": wg, "wu": wu, "wd": wd} for c in cores], core_ids=cores)
        return [np.asarray(res.results[c]["y"]) for c in cores]

    for l in range(ffn_norm.shape[0]):
        xT = run_ffn(xT, l, 0)
        nc1 = _prog("m1", build_m1_prog)
        g = colvec(mix_norm[l])
        res = run_bass_kernel_spmd(nc1, [{"x": xT[c], "g": g} for c in cores], core_ids=cores)
        h = [np.asarray(res.results[c]["h"]) for c in cores]
        nc2 = _prog("m2", build_m2_prog)
        wt = tile_w_in(w_in[l], D)
        in2 = []
        for c in cores:
            b, s = c // 2, c % 2
            d = {"hfull": np.ascontiguousarray(np.concatenate([h[2 * b], h[2 * b + 1]], axis=2)),
                 "w_in": np.ascontiguousarray(wt[w_in_chunk_ids(s)])}
            d["a_cosT"] = hc["cosT"]; d["a_sinT"] = hc["sinT"]; d["a_rotm"] = hc["rotm"]; d["a_ident_b"] = hc["ident"]
            d["a_maskb"] = hc["maskb"]; d["a_gq"] = np.ascontiguousarray(q_norm[l][:, None]); d["a_gk"] = np.ascontiguousarray(k_norm[l][:, None])
            d["a_gattn"] = np.ascontiguousarray(attn_out_norm[l][s * 1024:(s + 1) * 1024].reshape(8, 128).T)
            hd = hyena_host_inputs(s, conv_w[l], conv_b[l], filt_w1[l], filt_b1[l], filt_w2[l], filt_b2[l], filt_freq[l],
                                   filt_w3[l], filt_bias[l], hyena_out_norm[l], hc)
            for k, v in hd.items():
                d["h_" + k] = v
            in2.append(d)
        res = run_bass_kernel_spmd(nc2, in2, core_ids=cores)
        mg = [np.asarray(res.results[c]["merged"]) for c in cores]
        nc3 = _prog("m3", build_m3_prog)
        wo = tile_w_out(w_out[l], D, D)
        in3 = []
        for c in cores:
            b, th = c // 2, c % 2
            m0_, m1_ = mg[2 * b], mg[2 * b + 1]
            full = np.concatenate([m0_[0:8], m1_[0:8], m0_[8:16], m1_[8:16]], axis=0)
            in3.append({"x": xT[c], "mg": np.ascontiguousarray(full[:, :, th * 1024:(th + 1) * 1024]), "w_out": wo})
        res = run_bass_kernel_spmd(nc3, in3, core_ids=cores)
        xT = [np.asarray(res.results[c]["y"]) for c in cores]
        xT = run_ffn(xT, l, 1)
    out = np.empty((B, S, D), f32)
    for c in cores:
        out[c // 2, (c % 2) * 1024:(c % 2 + 1) * 1024, :] = xT[c].T
    return out
```
